# Optimizing a Trainium2 kernel written in Bass

```python
import jax, jax.numpy as jnp
from jax import lax
import numpy as np

D_MODEL = 2048
BATCH = 4
SEQ = 2048
DEPTH = 2

CHUNK = 64
EPS = 1e-6
D_POOL = D_MODEL // 2
POOL_WINDOWS = (2, 4, 8, 16)
N_POOL_GROUPS = len(POOL_WINDOWS)
POOL_GROUP = D_POOL // N_POOL_GROUPS
D_CONV = D_MODEL // 2
CONV_K = 31
D_AB_IN = D_POOL + 2 * D_CONV
D_SHORT = D_MODEL
SHORT_K = 3
D_FF = 4 * D_MODEL

N_EVEN = (DEPTH + 1) // 2
N_ODD = DEPTH // 2

kernel_name = "hybrid_pool_conformer_shortconv_trunk"


def rms_norm(x, g):
    xf = x.astype(jnp.float32)
    y = xf * lax.rsqrt(jnp.mean(xf * xf, axis=-1, keepdims=True) + EPS)
    return (y * g.astype(jnp.float32)).astype(x.dtype)


def layer_norm(x, g, b):
    xf = x.astype(jnp.float32)
    mu = jnp.mean(xf, axis=-1, keepdims=True)
    xc = xf - mu
    var = jnp.mean(xc * xc, axis=-1, keepdims=True)
    y = xc * lax.rsqrt(var + EPS) * g.astype(jnp.float32) + b.astype(jnp.float32)
    return y.astype(x.dtype)


def causal_depthwise_conv(u, w):
    k = w.shape[0]
    return lax.conv_general_dilated(
        u, w[:, None, :].astype(u.dtype), window_strides=(1,), padding=[(k - 1, 0)],
        dimension_numbers=("NWC", "WIO", "NWC"), feature_group_count=u.shape[-1])


def multiscale_pool(u, pool_w, pool_scale):
    b, t, _ = u.shape
    uf = u.astype(jnp.float32)
    csp = jnp.pad(jnp.cumsum(uf, axis=1), ((0, 0), (1, 0), (0, 0)))
    n_valid = jnp.arange(1, t + 1, dtype=jnp.float32)
    means = []
    for g, w in enumerate(POOL_WINDOWS):
        c = csp[..., g * POOL_GROUP:(g + 1) * POOL_GROUP]
        lag = jnp.pad(c, ((0, 0), (w - 1, 0), (0, 0)))[:, :t]
        cnt = jnp.minimum(n_valid, float(w))[None, :, None]
        means.append((c[:, 1:] - lag) / cnt)
    pooled = (jnp.concatenate(means, axis=-1) - uf).astype(u.dtype)
    pooled = pooled.reshape(b, t, N_POOL_GROUPS, POOL_GROUP)
    mixed = jnp.einsum("btgc,gce->btge", pooled, pool_w).reshape(b, t, D_POOL)
    return mixed * pool_scale


def pool_conformer_mixer(h, w_in, pool_w, pool_scale, conv_w, conv_b, ln_g, ln_b, w_out):
    z = jnp.einsum("btd,de->bte", h, w_in)
    u_pool = z[..., :D_POOL]
    v = z[..., D_POOL:D_POOL + D_CONV]
    gate = z[..., D_POOL + D_CONV:]
    y_pool = multiscale_pool(u_pool, pool_w, pool_scale)
    c = causal_depthwise_conv(v * jax.nn.sigmoid(gate), conv_w) + conv_b
    y_conv = jax.nn.silu(layer_norm(c, ln_g, ln_b))
    y = jnp.concatenate([y_pool, y_conv], axis=-1)
    return jnp.einsum("bte,ed->btd", y, w_out)


def short_conv_mixer(h, w_in, conv_w, w_out):
    z = jnp.einsum("btd,de->bte", h, w_in)
    b_gate = z[..., :D_SHORT]
    c_gate = z[..., D_SHORT:2 * D_SHORT]
    u = z[..., 2 * D_SHORT:]
    y = b_gate * causal_depthwise_conv(c_gate * u, conv_w)
    return jnp.einsum("bte,ed->btd", y, w_out)


def sq_relu_mlp(h, w1, w2):
    a = jax.nn.relu(jnp.einsum("btd,df->btf", h, w1))
    return jnp.einsum("btf,fd->btd", a * a, w2)


def setup_inputs(seed: int = 0) -> dict:
    key = jax.random.key(seed)
    ks = jax.random.split(key, 20)
    f32 = jnp.float32

    def nrm(k, shape, scale):
        return jax.random.normal(k, shape, f32) * scale

    def gain(k, shape):
        return 1.0 + 0.02 * jax.random.normal(k, shape, f32)

    return {
        "x": jax.random.normal(ks[0], (BATCH, SEQ, D_MODEL), f32),
        "mix_pre_g": gain(ks[1], (DEPTH, D_MODEL)),
        "mix_post_g": gain(ks[2], (DEPTH, D_MODEL)),
        "ffn_pre_g": gain(ks[3], (DEPTH, D_MODEL)),
        "ffn_post_g": gain(ks[4], (DEPTH, D_MODEL)),
        "ab_w_in": nrm(ks[5], (N_EVEN, D_MODEL, D_AB_IN), D_MODEL ** -0.5),
        "pool_w": nrm(ks[6], (N_EVEN, N_POOL_GROUPS, POOL_GROUP, POOL_GROUP), POOL_GROUP ** -0.5),
        "pool_scale": 1.0 + 0.1 * jax.random.normal(ks[7], (N_EVEN, D_POOL), f32),
        "conv_w": nrm(ks[8], (N_EVEN, CONV_K, D_CONV), CONV_K ** -0.5),
        "conv_b": nrm(ks[9], (N_EVEN, D_CONV), 0.02),
        "conv_ln_g": gain(ks[10], (N_EVEN, D_CONV)),
        "conv_ln_b": nrm(ks[11], (N_EVEN, D_CONV), 0.02),
        "ab_w_out": nrm(ks[12], (N_EVEN, D_POOL + D_CONV, D_MODEL), (D_POOL + D_CONV) ** -0.5),
        "sc_w_in": nrm(ks[13], (N_ODD, D_MODEL, 3 * D_SHORT), D_MODEL ** -0.5),
        "sc_conv_w": nrm(ks[14], (N_ODD, SHORT_K, D_SHORT), SHORT_K ** -0.5),
        "sc_w_out": nrm(ks[15], (N_ODD, D_SHORT, D_MODEL), D_SHORT ** -0.5),
        "ffn_w1": nrm(ks[16], (DEPTH, D_MODEL, D_FF), D_MODEL ** -0.5),
        "ffn_w2": nrm(ks[17], (DEPTH, D_FF, D_MODEL), D_FF ** -0.5),
    }


def reference(x, mix_pre_g, mix_post_g, ffn_pre_g, ffn_post_g, ab_w_in, pool_w, pool_scale,
              conv_w, conv_b, conv_ln_g, conv_ln_b, ab_w_out, sc_w_in, sc_conv_w, sc_w_out,
              ffn_w1, ffn_w2):
    for layer in range(DEPTH):
        i = layer // 2
        h = rms_norm(x, mix_pre_g[layer])
        if layer % 2 == 0:
            m = pool_conformer_mixer(h, ab_w_in[i], pool_w[i], pool_scale[i], conv_w[i], conv_b[i],
                                     conv_ln_g[i], conv_ln_b[i], ab_w_out[i])
        else:
            m = short_conv_mixer(h, sc_w_in[i], sc_conv_w[i], sc_w_out[i])
        x = x + rms_norm(m, mix_post_g[layer])
        h = rms_norm(x, ffn_pre_g[layer])
        x = x + rms_norm(sq_relu_mlp(h, ffn_w1[layer], ffn_w2[layer]), ffn_post_g[layer])
    return x
```

```python
import bisect
from contextlib import ExitStack

import numpy as np
import concourse.bass as bass
import concourse.mybir as mybir
from concourse.bass_utils import run_bass_kernel_spmd

F32 = mybir.dt.float32
BF16 = mybir.dt.bfloat16
AF = mybir.ActivationFunctionType
ALU = mybir.AluOpType

D = 2048
KC = 16
SEQ = 2048
HALO = 32
TOWN = 1024
T = TOWN + HALO
EPS = 1e-6
NCORES = 8
CELL = 576
NCELL = 40
CW = 256
NSLOT = 4

PASSES = [
    (0, 544, [(0, 32, "H"), (32, 512, "M")]),
    (544, 512, [(0, 512, "M")]),
]

C_MIXPRE, C_MIXPOST, C_FFNPRE, C_FFNPOST = 0, 32, 64, 96
C_POOLSC = 128
C_CONVW = 136
C_CONVB = 384
C_LNG = 392
C_LNB = 400
C_SCW = 408
C_INVFIX = 456
C_MASK = 520
NCST = 528


class Op:
    __slots__ = ("gidx", "eng", "emit", "deps", "dsem", "flag", "tok", "incs", "waits", "pos")

    def __init__(self, gidx, eng, emit, deps, dsem, flag):
        self.gidx = gidx
        self.eng = eng
        self.emit = emit
        self.deps = deps
        self.dsem = dsem
        self.flag = flag
        self.tok = None
        self.incs = False
        self.waits = []
        self.pos = -1


class Builder:
    def __init__(self):
        self.ops = []
        self.state = {}
        self.acc_ctr = 0
        self.h_ctr = 0
        self.st_ctr = 0
        self.hs_ctr = 0
        self.slab_ctr = 0
        self.dma_counts = {}

    def add(self, eng, emit, reads=(), writes=(), dsem=None, flag=False, nowaw=False):
        deps = set()
        for k in reads:
            st = self.state.get(k)
            if st is not None and st[0] is not None:
                deps.add(st[0])
        for k in writes:
            st = self.state.get(k)
            if st is not None:
                if st[0] is not None and not (nowaw and self.ops[st[0]].eng == eng and self.ops[st[0]].dsem is None):
                    deps.add(st[0])
                deps.update(st[1].values())
                deps.update(st[2])
        g = len(self.ops)
        op = Op(g, eng, emit, deps, dsem, flag)
        self.ops.append(op)
        for k in reads:
            st = self.state.setdefault(k, [None, {}, []])
            if dsem is not None:
                st[2].append(g)
            else:
                st[1][eng] = g
        for k in writes:
            self.state[k] = [g, {}, []]
        return op

    def finalize(self, sems):
        ops = self.ops
        byeng = {}
        for op in ops:
            if op.dsem is None:
                lst = byeng.setdefault(op.eng, [])
                op.pos = len(lst)
                lst.append(op)
        pe_ops = byeng.get("pe", [])
        flagged = [op.pos for op in pe_ops if op.flag]
        for op in ops:
            if op.eng == "pe" and op.dsem is None:
                continue
            for d in op.deps:
                dop = ops[d]
                if dop.eng == "pe" and dop.dsem is None:
                    i = bisect.bisect_left(flagged, dop.pos)
                    if i < len(flagged) and pe_ops[flagged[i]].gidx < op.gidx:
                        continue
                    bisect.insort(flagged, dop.pos)
        flagged_set = set(flagged)
        for eng, lst in byeng.items():
            cnt = 0
            for op in lst:
                if eng != "pe" or op.pos in flagged_set:
                    cnt += 1
                    op.tok = (sems[eng], cnt)
                    op.incs = True
        dcount = {}
        for op in ops:
            if op.dsem is not None:
                dcount[op.dsem] = dcount.get(op.dsem, 0) + 1
                op.tok = (op.dsem, 16 * dcount[op.dsem])
                op.incs = True
        for op in ops:
            if op.dsem is not None and op.dsem[1] == "group":
                op.tok = (op.dsem, 16 * dcount[op.dsem])

        def token_of(dop, dependent_gidx):
            if dop.tok is not None:
                return dop.tok, dop
            i = bisect.bisect_left(flagged, dop.pos)
            f = pe_ops[flagged[i]]
            assert f.gidx < dependent_gidx
            return f.tok, f

        known = {}
        snaps = {}
        for op in ops:
            kn = known.setdefault(op.eng, {})
            for d in sorted(op.deps):
                dop = ops[d]
                if op.eng == "pe" and dop.eng == "pe" and dop.dsem is None and op.dsem is None:
                    continue
                tok, src = token_of(dop, op.gidx)
                sem, val = tok
                if kn.get(sem, 0) >= val:
                    continue
                op.waits.append((sem, val))
                kn[sem] = val
                sn = snaps.get(src.gidx)
                if sn:
                    for k, v in sn.items():
                        if kn.get(k, 0) < v:
                            kn[k] = v
            if op.incs:
                sn = dict(kn)
                if op.dsem is None:
                    sn[op.tok[0]] = max(sn.get(op.tok[0], 0), op.tok[1])
                else:
                    sn[op.tok[0]] = max(sn.get(op.tok[0], 0), op.tok[1])
                snaps[op.gidx] = sn
        return byeng


def build_program(debug=False, limit=None):
    nc = bass.Bass("TRN2", target_bir_lowering=False)
    B = Builder()
    dbg_d = nc.dram_tensor("dbg", [128, KC, 544], F32, kind="ExternalOutput").ap() if debug else None

    def din(name, shape):
        return nc.dram_tensor(name, list(shape), F32, kind="ExternalInput").ap()

    x_d = din("x_fm", (D, T))
    cst_d = din("cst", (128, NCST))
    poolw_d = din("pool_w", (4, 256, 256))
    ident_d = din("ident", (128, 128))
    w_abin = din("ab_w_in", (D, 3072))
    w_about = din("ab_w_out", (D, D))
    w_scin = din("sc_w_in", (D, 6144))
    w_scout = din("sc_w_out", (D, D))
    w_f1 = [din("ffn_w1_%d" % l, (D, 8192)) for l in range(2)]
    w_f2 = [din("ffn_w2_%d" % l, (8192, D)) for l in range(2)]
    out_d = nc.dram_tensor("out_fm", [D, TOWN], F32, kind="ExternalOutput").ap()

    es = ExitStack()
    with es:
        xs = es.enter_context(nc.sbuf_tensor("xs", [128, KC, T], F32))
        arena = es.enter_context(nc.sbuf_tensor("arena", [128, NCELL * CELL], F32))
        wbuf = es.enter_context(nc.sbuf_tensor("wbuf", [128, NSLOT, KC, CW], BF16))
        cst = es.enter_context(nc.sbuf_tensor("cst_sb", [128, NCST], F32))
        poolw = es.enter_context(nc.sbuf_tensor("poolw_sb", [128, 4, 2, 256], BF16))
        ones = es.enter_context(nc.sbuf_tensor("ones_sb", [128, 128], BF16))
        epsc = es.enter_context(nc.sbuf_tensor("eps_sb", [128, 1], F32))
        tail_glu = es.enter_context(nc.sbuf_tensor("tail_glu", [128, 8, 32], BF16))
        ident = es.enter_context(nc.sbuf_tensor("ident_sb", [128, 128], BF16))
        dg = es.enter_context(nc.sbuf_tensor("dg_sb", [128, 2, 16, 128], BF16))
        zeros = es.enter_context(nc.sbuf_tensor("zeros_sb", [128, 512], BF16))
        tail_u = es.enter_context(nc.sbuf_tensor("tail_u", [128, 8, 16], F32))
        tail_cu = es.enter_context(nc.sbuf_tensor("tail_cu", [128, 16, 2], F32))
        pacc = es.enter_context(nc.psum_tensor("pacc", [128, 3, 512], F32))
        ph = es.enter_context(nc.psum_tensor("ph", [128, 2, 512], F32))
        pst = es.enter_context(nc.psum_tensor("pst", [128, 1, 512], F32))
        pwarm = es.enter_context(nc.psum_tensor("pwarm", [128, 512], F32))
        phs = es.enter_context(nc.psum_tensor("phs", [128, 512], F32))

        sem_names = ["pe", "act", "dve", "pool", "sp"]
        sems = {}
        for n_ in sem_names:
            sems[n_] = es.enter_context(nc.semaphore("s_" + n_))
        dsem_handles = {}

        def dsem(name, mode="slot"):
            key = (name, mode)
            if key not in dsem_handles:
                dsem_handles[key] = es.enter_context(nc.semaphore("d_" + name))
            return key

        arena_bf = arena[:].bitcast(BF16)

        def cf(cell, a=0, b=CELL):
            return arena[:, cell * CELL + a: cell * CELL + b]

        def ch(hidx, a=0, b=CELL):
            return arena_bf[:, hidx * CELL + a: hidx * CELL + b]

        def kf(cell):
            return [("c", cell, 0), ("c", cell, 1)]

        def kh(hidx):
            return [("c", hidx // 2, hidx % 2)]

        HB0, Y0 = 0, 16
        MB0 = 16
        EX0 = 32

        def hb_ap(c, a, b):
            return ch(HB0 + c, a, b)

        def y_ap(c, a, b):
            return ch(Y0 + c, a, b)

        def mb_ap(c, a, b):
            return cf(MB0 + c, a, b)

        def cs(col):
            return cst[:, col:col + 1]

        def act(out, in_, func, reads, writes, bias=None, scale=None):
            kw = {}
            if bias is not None:
                kw["bias"] = bias
            if scale is not None:
                kw["scale"] = scale
            B.add("act", lambda e: e.activation(out=out, in_=in_, func=func, **kw), reads, writes)

        def tt(out, in0, in1, op, reads, writes, eng="dve"):
            B.add(eng, lambda e: e.tensor_tensor(out=out, in0=in0, in1=in1, op=op), reads, writes)

        def stt(out, in0, scalar, in1, op0, op1, reads, writes, eng="dve"):
            B.add(eng, lambda e: e.scalar_tensor_tensor(out=out, in0=in0, scalar=scalar, in1=in1,
                                                         op0=op0, op1=op1), reads, writes)

        def ts(out, in0, s1, s2, op0, op1, reads, writes, eng="dve"):
            if s2 is None:
                B.add(eng, lambda e: e.tensor_scalar(out=out, in0=in0, scalar1=s1, scalar2=None, op0=op0),
                      reads, writes)
            else:
                B.add(eng, lambda e: e.tensor_scalar(out=out, in0=in0, scalar1=s1, scalar2=s2,
                                                     op0=op0, op1=op1), reads, writes)

        def cp(out, in_, reads, writes, eng="dve"):
            B.add(eng, lambda e: e.tensor_copy(out=out, in_=in_), reads, writes)

        def mm(out, lhsT, rhs, start, stop, reads, writes, flag=False):
            B.add("pe", lambda e: e.matmul(out, lhsT, rhs, start=start, stop=stop), reads, writes, flag=flag)

        def warm(n):
            for _ in range(n):
                mm(pwarm[:, :], ones[:], zeros[:], True, True, ["ones", "zeros"], [])

        def dma(queue, out, in_, sem, reads, writes):
            B.add(queue, lambda e: e.dma_start(out=out, in_=in_), reads, writes, dsem=sem)

        B.add("dve", lambda e: e.memset(arena[:, 0:NCELL * CELL // 2], 0.0), [],
              [k for c in range(NCELL // 2) for k in kf(c)])
        B.add("dve", lambda e: e.memset(arena[:, NCELL * CELL // 2:], 0.0), [],
              [k for c in range(NCELL // 2, NCELL) for k in kf(c)])
        B.add("dve", lambda e: e.memset(ones[:], 1.0), [], ["ones"])
        B.add("dve", lambda e: e.memset(zeros[:], 0.0), [], ["zeros"])
        B.add("dve", lambda e: e.memset(epsc[:], EPS), [], ["cst"])
        B.add("dve", lambda e: e.memset(tail_glu[:], 0.0), [], [("tg", j) for j in range(8)])
        B.add("dve", lambda e: e.memset(tail_u[:], 0.0), [], [("tu", j) for j in range(8)])
        B.add("dve", lambda e: e.memset(tail_cu[:], 0.0), [], [("tc", j) for j in range(16)])
        sx = dsem("xin", "group")
        dma("sp", cst[:], cst_d, sx, [], ["cst"])
        xv = x_d.rearrange("(c p) t -> p c t", p=128)
        for q in range(4):
            dma("sp", xs[:, 4 * q:4 * q + 4, :], xv[:, 4 * q:4 * q + 4, :], sx, [],
                [("x", c, p) for c in range(4 * q, 4 * q + 4) for p in range(2)])
        dma("pool", poolw[:], poolw_d.rearrange("g (cc p) e -> p g cc e", p=128), dsem("pw", "group"), [], ["poolw"])
        dma("pool", ident[:], ident_d, dsem("pw", "group"), [], ["ident"])

        def tiles_for(l, p, full):
            if full or not (l == 1 and p == 0):
                return PASSES[p][2]
            return [t for t in PASSES[p][2] if t[2] == "M"]

        def trange(tiles):
            return min(t[0] for t in tiles), max(t[0] + t[1] for t in tiles)

        def alloc_acc(tiles):
            outs = []
            for (off, n, kind) in tiles:
                if kind == "M":
                    s = B.acc_ctr % 3
                    B.acc_ctr += 1
                    outs.append((pacc[:, s, 0:n], ("acc", s), off, n))
                else:
                    s = B.h_ctr % 2
                    B.h_ctr += 1
                    outs.append((ph[:, s, 0:n], ("ph", s), off, n))
            return outs

        def alloc_stat(tiles, alt=False):
            outs = []
            for (off, n, kind) in tiles:
                if kind == "M" and alt:
                    s = B.acc_ctr % 3
                    B.acc_ctr += 1
                    outs.append((pacc[:, s, 0:n], ("acc", s), off, n))
                elif kind == "M":
                    outs.append((pst[:, 0, 0:n], ("pst", 0), off, n))
                elif alt:
                    s = B.h_ctr % 2
                    B.h_ctr += 1
                    outs.append((ph[:, s, 0:n], ("ph", s), off, n))
                else:
                    outs.append((phs[:, 0:n], ("phs", 0), off, n))
            return outs

        def load_slab(W, r0, c0):
            slot = B.slab_ctr % NSLOT
            B.slab_ctr += 1
            src = W[r0:r0 + D, c0:c0 + CW].rearrange("(kc p) n -> p kc n", p=128)
            dma("pool", wbuf[:, slot, :, :], src, dsem("w%d" % slot), [], [("w", slot)])
            return slot

        def mm_stage(W, r0, cols, rhs_fn, rhs_keys, tiles, evac):
            deferred = []
            mi_g = 0
            for c0 in cols:
                slot = load_slab(W, r0, c0)
                for mi in range(CW // 128):
                    outs = alloc_acc(tiles(c0 + mi * 128) if callable(tiles) else tiles)
                    for k in range(KC):
                        for (oap, okey, off, n) in outs:
                            mm(oap, wbuf[:, slot, k, mi * 128:(mi + 1) * 128], rhs_fn(k, off, n),
                               k == 0, k == KC - 1, [("w", slot)] + rhs_keys(k), [okey], flag=(k == KC - 1))
                    nxt = evac(mi_g, c0 + mi * 128, outs) or []
                    for th in deferred:
                        th()
                    deferred = nxt
                    mi_g += 1
            for th in deferred:
                th()

        SQ = [2 * EX0, 2 * EX0 + 1, 2 * (EX0 + 1)]
        sq_ctr = [0]

        def next_sq():
            h = SQ[sq_ctr[0] % 3]
            sq_ctr[0] += 1
            return h

        def stat_accum(stat, src_h, first, last):
            for (oap, okey, off, n) in stat:
                mm(oap, ones[:], ch(src_h, off, off + n), first, last, ["ones"] + kh(src_h), [okey], flag=last)

        def rstd_from(stat, cell, scale, eps=EPS):
            for (oap, okey, off, n) in stat:
                act(cf(cell, off, off + n), oap, AF.Sqrt, [okey, "cst"], kf(cell), bias=epsc[:, 0:1], scale=scale)
            lo = min(t[2] for t in stat)
            hi = max(t[2] + t[3] for t in stat)
            B.add("dve", lambda e: e.reciprocal(out=cf(cell, lo, hi), in_=cf(cell, lo, hi)), kf(cell), kf(cell))

        def prenorm_stats(l, gbase, p, full=True, wpre=0, wmid=0, wpost=0):
            g0 = PASSES[p][0]
            tiles = tiles_for(l, p, full)
            lo, hi = trange(tiles)
            stat = alloc_stat(tiles)
            warm(wpre)
            for c in range(KC):
                h = next_sq()
                act(ch(h, lo, hi), xs[:, c, g0 + lo:g0 + hi], AF.Square, [("x", c, p)], kh(h))
                stat_accum(stat, h, c == 0, c == KC - 1)
                warm(wmid)
            warm(wpost)
            rstd_from(stat, EX0 + 3, 1.0 / D)

        def prenorm_hb(l, gbase, p, full=True):
            g0 = PASSES[p][0]
            lo, hi = trange(tiles_for(l, p, full))
            rc = EX0 + 3
            for c in range(KC):
                stt(hb_ap(c, lo, hi), xs[:, c, g0 + lo:g0 + hi], cs(gbase + l * 16 + c), cf(rc, lo, hi),
                    ALU.mult, ALU.mult, [("x", c, p), "cst"] + kf(rc), kh(HB0 + c))

        def prenorm(l, gbase, p, full=True, wpre=0, wmid=0, wpost=0):
            prenorm_stats(l, gbase, p, full, wpre, wmid, wpost)
            prenorm_hb(l, gbase, p, full)

        def post_update(l, gbase, p, stat):
            g0 = PASSES[p][0]
            lo = min(t[2] for t in stat)
            hi = max(t[2] + t[3] for t in stat)
            rc = EX0 + 2
            rstd_from(stat, rc, 1.0 / D)
            for c in range(KC):
                tcell = EX0 + 4 + (c % 2)
                stt(cf(tcell, lo, hi), mb_ap(c, lo, hi), cs(gbase + l * 16 + c), cf(rc, lo, hi), ALU.mult, ALU.mult,
                    kf(MB0 + c) + ["cst"] + kf(rc), kf(tcell))
                tt(xs[:, c, g0 + lo:g0 + hi], cf(tcell, lo, hi), xs[:, c, g0 + lo:g0 + hi], ALU.add,
                   kf(tcell) + [("x", c, p)], [("x", c, p)])

        def hb_rhs(k, off, n):
            return hb_ap(k, off, off + n)

        def hb_keys(k):
            return kh(HB0 + k)

        def y_rhs(k, off, n):
            return y_ap(k, off, off + n)

        def y_keys(k):
            return kh(Y0 + k)

        def outproj(l, W, p):
            tiles = tiles_for(l, p, False)
            stat = alloc_stat(tiles)

            def evac(mi, c0, outs):
                h = next_sq()
                for (oap, okey, off, nn) in outs:
                    act(mb_ap(mi, off, off + nn), oap, AF.Copy, [okey], kf(MB0 + mi))
                    act(ch(h, off, off + nn), oap, AF.Square, [okey], kh(h))
                return [lambda: stat_accum(stat, h, mi == 0, mi == KC - 1)]

            mm_stage(W, 0, [b * CW for b in range(D // CW)], y_rhs, y_keys, tiles, evac)
            post_update(l, C_MIXPOST, p, stat)

        def ffn(l, p, nxt=None):
            tiles = tiles_for(l, p, False)
            lo, hi = trange(tiles)
            stat = alloc_stat(tiles)
            if nxt is not None:
                ng0 = PASSES[nxt[1]][0]
                ntiles = tiles_for(nxt[0], nxt[1], True)
                nlo, nhi = trange(ntiles)
            for q in range(4):
                if q == 2 and nxt is not None:
                    nstat = alloc_stat(ntiles)

                def evac1(mi, c0, outs, q=q):
                    rc_ = EX0 + 4 + (mi % 2)
                    for (oap, okey, off, nn) in outs:
                        act(cf(rc_, off, off + nn), oap, AF.Relu, [okey], kf(rc_))
                    tt(y_ap(mi, lo, hi), cf(rc_, lo, hi), cf(rc_, lo, hi), ALU.mult, kf(rc_), kh(Y0 + mi))
                    if q == 2 and nxt is not None:
                        h = next_sq()
                        act(ch(h, nlo, nhi), xs[:, mi, ng0 + nlo:ng0 + nhi], AF.Square, [("x", mi, nxt[1])], kh(h))
                        return [lambda: stat_accum(nstat, h, mi == 0, mi == KC - 1)]
                    return None

                mm_stage(w_f1[l], 0, [q * D + b * CW for b in range(D // CW)], hb_rhs, hb_keys, tiles, evac1)
                if q == 2 and nxt is not None:
                    rstd_from(nstat, EX0 + 3, 1.0 / D)

                def evac2(mi, c0, outs, q=q):
                    ths = None
                    for (oap, okey, off, nn) in outs:
                        if q == 0:
                            act(mb_ap(mi, off, off + nn), oap, AF.Copy, [okey], kf(MB0 + mi))
                        else:
                            tt(mb_ap(mi, off, off + nn), oap, mb_ap(mi, off, off + nn), ALU.add,
                               [okey] + kf(MB0 + mi), kf(MB0 + mi))
                    if q == 3:
                        h = next_sq()
                        act(ch(h, lo, hi), mb_ap(mi, lo, hi), AF.Square, kf(MB0 + mi), kh(h))
                        ths = [lambda: stat_accum(stat, h, mi == 0, mi == KC - 1)]
                        if nxt is not None:
                            stt(hb_ap(mi, nlo, nhi), xs[:, mi, ng0 + nlo:ng0 + nhi],
                                cs(C_MIXPRE + nxt[0] * 16 + mi), cf(EX0 + 3, nlo, nhi), ALU.mult, ALU.mult,
                                [("x", mi, nxt[1]), "cst"] + kf(EX0 + 3), kh(HB0 + mi))
                    return ths

                mm_stage(w_f2[l], q * D, [b * CW for b in range(D // CW)], y_rhs, y_keys, tiles, evac2)
            post_update(l, C_FFNPOST, p, stat)

        def mixer0(p):
            g0, n, tiles = PASSES[p]
            GLH = [2 * (MB0 + 12), 2 * (MB0 + 12) + 1]
            SG = [MB0 + 14, MB0 + 15]
            UA = EX0 + 6
            SA, SB = EX0 + 4, EX0 + 5

            def build_diag(j, hh):
                for sl in range(16):
                    k = hh * 16 + sl
                    if k > 30:
                        break
                    B.add("dve", (lambda e, sl=sl, k=k: e.tensor_scalar(
                        out=dg[:, hh, sl, :], in0=ident[:], scalar1=cs(C_CONVW + k * 8 + j), scalar2=None,
                        op0=ALU.mult)), ["ident", "cst"], [("dg", hh)], nowaw=True)

            def pooled_h(j):
                return 2 * (MB0 + 8) + j

            def evac(mi, c0, outs):
                ths = None
                if c0 >= 2048:
                    j = (c0 - 2048) // 128
                    sc = SG[j % 2]
                    for (oap, okey, off, nn) in outs:
                        act(cf(sc, off, off + nn), oap, AF.Sigmoid, [okey], kf(sc))
                elif c0 >= 1024:
                    j = (c0 - 1024) // 128
                    sc = SG[j % 2]
                    gh = GLH[j % 2]
                    if p == 1:
                        cp(ch(gh, 0, 32), tail_glu[:, j, :], [("tg", j)], kh(gh))
                    for (oap, okey, off, nn) in outs:
                        tt(ch(gh, 32 + off, 32 + off + nn), oap, cf(sc, off, off + nn), ALU.mult,
                           [okey] + kf(sc), kh(gh))
                    if p == 0:
                        cp(tail_glu[:, j, :], ch(gh, n, n + 32), kh(gh), [("tg", j)])
                    if j == 0:
                        build_diag(0, 0)
                        build_diag(0, 1)

                    def conv_mm(j=j, gh=gh):
                        outs2 = alloc_acc(tiles)
                        for k in range(31):
                            hh, sl = k // 16, k % 16
                            for (oap, okey, off, nn) in outs2:
                                mm(oap, dg[:, hh, sl, :], ch(gh, off + 2 + k, off + 2 + k + nn), k == 0, k == 30,
                                   [("dg", hh)] + kh(gh), [okey], flag=(k == 30 or k == 15))
                            if j < 7 and (k == 15 or k == 30):
                                build_diag(j + 1, hh)
                        for (oap, okey, off, nn) in outs2:
                            act(mb_ap(j, off, off + nn), oap, AF.Identity, [okey, "cst"], kf(MB0 + j),
                                bias=cs(C_CONVB + j))
                    ths = [conv_mm]
                else:
                    j = c0 // 128
                    g = j // 2
                    w = 2 << g
                    if p == 1:
                        cp(cf(UA, 0, 16), tail_u[:, j, :], [("tu", j)], kf(UA))
                    for (oap, okey, off, nn) in outs:
                        act(cf(UA, 16 + off, 16 + off + nn), oap, AF.Copy, [okey], kf(UA))
                    if p == 0:
                        cp(tail_u[:, j, :], cf(UA, n, n + 16), kf(UA), [("tu", j)])
                    src = UA
                    lo = 0
                    sh = 1
                    dsts = [SA, SB]
                    di = 0
                    while sh < w:
                        dst = dsts[di % 2]
                        di += 1
                        lo2 = lo + sh
                        tt(cf(dst, lo2, 16 + n), cf(src, lo2, 16 + n), cf(src, lo2 - sh, 16 + n - sh), ALU.add,
                           kf(src), kf(dst))
                        src = dst
                        lo = lo2
                        sh *= 2
                    hp = pooled_h(j)
                    stt(ch(hp, 0, n), cf(src, 16, 16 + n), 1.0 / w, cf(UA, 16, 16 + n), ALU.mult, ALU.subtract,
                        kf(src) + kf(UA), kh(hp))
                    if p == 0:
                        tmp = dsts[di % 2]
                        tt(cf(tmp, 0, 15), cf(src, 16 + 32, 16 + 47), cst[:, C_INVFIX + g * 16:C_INVFIX + g * 16 + 15],
                           ALU.mult, kf(src) + ["cst"], kf(tmp))
                        tt(ch(hp, 32, 47), cf(tmp, 0, 15), cf(UA, 16 + 32, 16 + 47), ALU.subtract,
                           kf(tmp) + kf(UA), kh(hp))
                    if j % 2 == 1:
                        def pool_mm(g=g):
                            for e in range(2):
                                outs2 = alloc_acc(tiles)
                                for cc in range(2):
                                    hpp = pooled_h(2 * g + cc)
                                    for (oap, okey, off, nn) in outs2:
                                        mm(oap, poolw[:, g, cc, e * 128:(e + 1) * 128], ch(hpp, off, off + nn),
                                           cc == 0, cc == 1, ["poolw"] + kh(hpp), [okey], flag=(cc == 1))
                                yi = 2 * g + e
                                for (oap, okey, off, nn) in outs2:
                                    act(y_ap(yi, off, off + nn), oap, AF.Identity, [okey, "cst"], kh(Y0 + yi),
                                        scale=cs(C_POOLSC + yi))
                        ths = [pool_mm]
                return ths

            cols = []
            for j2 in range(4):
                cols.append(2048 + j2 * CW)
                cols.append(1024 + j2 * CW)
            for j2 in range(4):
                cols.append(j2 * CW)
            mm_stage(w_abin, 0, cols, hb_rhs, hb_keys, tiles, evac)

            st_mean = alloc_stat(tiles)
            st_sq = alloc_stat(tiles, alt=True)
            for j in range(8):
                h1 = next_sq()
                act(ch(h1, 0, n), mb_ap(j, 0, n), AF.Copy, kf(MB0 + j), kh(h1))
                stat_accum(st_mean, h1, j == 0, j == 7)
                h2 = next_sq()
                act(ch(h2, 0, n), mb_ap(j, 0, n), AF.Square, kf(MB0 + j), kh(h2))
                stat_accum(st_sq, h2, j == 0, j == 7)
            MEAN, VAR, LR = EX0 + 2, EX0 + 3, EX0 + 7
            for (oap, okey, off, nn) in st_mean:
                act(cf(MEAN, off, off + nn), oap, AF.Copy, [okey], kf(MEAN), scale=1.0 / 1024)
            tt(cf(VAR, 0, n), cf(MEAN, 0, n), cf(MEAN, 0, n), ALU.mult, kf(MEAN), kf(VAR))
            for (oap, okey, off, nn) in st_sq:
                stt(cf(LR, off, off + nn), oap, 1.0 / 1024, cf(VAR, off, off + nn), ALU.mult, ALU.subtract,
                    [okey] + kf(VAR), kf(LR))
            act(cf(LR, 0, n), cf(LR, 0, n), AF.Sqrt, kf(LR) + ["cst"], kf(LR), bias=epsc[:, 0:1], scale=1.0)
            B.add("dve", lambda e: e.reciprocal(out=cf(LR, 0, n), in_=cf(LR, 0, n)), kf(LR), kf(LR))
            for j in range(8):
                tc_ = EX0 + 4 + (j % 2)
                tt(cf(tc_, 0, n), mb_ap(j, 0, n), cf(MEAN, 0, n), ALU.subtract, kf(MB0 + j) + kf(MEAN), kf(tc_))
                tt(cf(tc_, 0, n), cf(tc_, 0, n), cf(LR, 0, n), ALU.mult, kf(tc_) + kf(LR), kf(tc_))
                act(y_ap(8 + j, 0, n), cf(tc_, 0, n), AF.Silu, kf(tc_) + ["cst"], kh(Y0 + 8 + j),
                    bias=cs(C_LNB + j), scale=cs(C_LNG + j))

        def mixer1(p):
            g0, n, tiles = PASSES[p]
            CS = [MB0 + 0, MB0 + 1]
            CU = [MB0 + 2, MB0 + 3]
            CV = [MB0 + 4, MB0 + 5]

            def evac(mi, c0, outs):
                if c0 >= 4096:
                    j = (c0 - 4096) // 128
                    cc, cu, cv = CS[j % 2], CU[j % 2], CV[j % 2]
                    if p == 1:
                        cp(cf(cu, 0, 2), tail_cu[:, j, :], [("tc", j)], kf(cu))
                    for (oap, okey, off, nn) in outs:
                        tt(cf(cu, 2 + off, 2 + off + nn), oap, cf(cc, off, off + nn), ALU.mult,
                           [okey] + kf(cc), kf(cu))
                    if p == 0:
                        cp(tail_cu[:, j, :], cf(cu, n, n + 2), kf(cu), [("tc", j)])
                    ts(cf(cv, 0, n), cf(cu, 0, n), cs(C_SCW + j), None, ALU.mult, None, kf(cu) + ["cst"], kf(cv))
                    for k in range(1, 3):
                        stt(cf(cv, 0, n), cf(cu, k, k + n), cs(C_SCW + k * 16 + j), cf(cv, 0, n),
                            ALU.mult, ALU.add, kf(cu) + kf(cv) + ["cst"], kf(cv))
                elif c0 >= 2048:
                    j = (c0 - 2048) // 128
                    cc = CS[j % 2]
                    for (oap, okey, off, nn) in outs:
                        act(cf(cc, off, off + nn), oap, AF.Copy, [okey], kf(cc))
                else:
                    j = c0 // 128
                    cv = CV[j % 2]
                    for (oap, okey, off, nn) in outs:
                        tt(y_ap(j, off, off + nn), oap, cf(cv, off, off + nn), ALU.mult, [okey] + kf(cv),
                           kh(Y0 + j))
                return None

            cols = []
            for j2 in range(8):
                cols.append(2048 + j2 * CW)
                cols.append(4096 + j2 * CW)
                cols.append(j2 * CW)
            mtiles = [t for t in tiles if t[2] == "M"]
            mm_stage(w_scin, 0, cols, hb_rhs, hb_keys, (lambda c0: mtiles if c0 < 2048 else tiles), evac)

        so = dsem("out", "group")
        ov = out_d.rearrange("(c p) t -> p c t", p=128)
        def reached(l, p, st):
            return limit is not None and (l, p, st) > tuple(limit)

        for l in range(2):
            for p in range(2):
                g0, n, tiles = PASSES[p]
                if l == 0 and p == 0:
                    prenorm(l, C_MIXPRE, p)
                if not reached(l, p, 1):
                    if l == 0:
                        mixer0(p)
                    else:
                        mixer1(p)
                    if debug and p == 0 and l == 0:
                        for c in range(KC):
                            dma("pool", dbg_d[:, c, :], y_ap(c, 0, 544), dsem("dbg", "group"), kh(Y0 + c), [("dbg", c)])
                if not reached(l, p, 2):
                    outproj(l, w_about if l == 0 else w_scout, p)
                if not reached(l, p, 3):
                    prenorm(l, C_FFNPRE, p, full=False, wpre=30, wmid=4, wpost=55)
                if not reached(l, p, 4):
                    nxt = (l, 1) if p == 0 else ((l + 1, 0) if l == 0 else None)
                    ffn(l, p, nxt)
                if l == 0 and p == 0:
                    ts(xs[:, :, 0:HALO], xs[:, :, 0:HALO], cs(C_MASK), None, ALU.mult, None,
                       [("x", c, 0) for c in range(KC)] + ["cst"], [("x", c, 0) for c in range(KC)])
                if l == 1:
                    lo = HALO if p == 0 else g0
                    for q in range(4):
                        dma("sp", ov[:, 4 * q:4 * q + 4, lo - HALO:lo - HALO + 512], xs[:, 4 * q:4 * q + 4, lo:lo + 512],
                            so, [("x", c, p) for c in range(4 * q, 4 * q + 4)], [("out", p, q)])
        B.add("sp", None, [("out", p, q) for p in range(2) for q in range(4)]
              + ([("dbg", c) for c in range(KC)] if debug else []), ["done"])

        byeng = B.finalize({k: k for k in sem_names})

        def sem_of(s):
            if isinstance(s, tuple):
                return dsem_handles[s]
            return sems[s]

        def emit_engine(name, e):
            lst = [op for op in B.ops if op.eng == name]
            for op in lst:
                for (s, v) in op.waits:
                    e.wait_ge(sem_of(s), v)
                if op.emit is None:
                    continue
                ins = op.emit(e)
                if op.incs:
                    if op.dsem is not None:
                        ins.then_inc(sem_of(op.dsem), 16)
                    else:
                        ins.then_inc(sems[name], 1)

        with nc.Block() as block:
            @block.tensor
            def _(e):
                emit_engine("pe", e)

            @block.scalar
            def _(e):
                emit_engine("act", e)

            @block.vector
            def _(e):
                emit_engine("dve", e)

            @block.gpsimd
            def _(e):
                emit_engine("pool", e)

            @block.sync
            def _(e):
                emit_engine("sp", e)
    return nc


def _fm(v):
    v = np.asarray(v, dtype=np.float32)
    return np.ascontiguousarray(v.reshape(-1, 128).T)


def make_in_maps(inputs):
    x = np.asarray(inputs["x"], dtype=np.float32)
    cst = np.zeros((128, NCST), np.float32)
    for base, name in ((C_MIXPRE, "mix_pre_g"), (C_MIXPOST, "mix_post_g"), (C_FFNPRE, "ffn_pre_g"),
                       (C_FFNPOST, "ffn_post_g")):
        g = np.asarray(inputs[name], np.float32)
        for l in range(2):
            cst[:, base + l * 16: base + (l + 1) * 16] = _fm(g[l])
    cst[:, C_POOLSC:C_POOLSC + 8] = _fm(inputs["pool_scale"][0])
    cw = np.asarray(inputs["conv_w"], np.float32)[0]
    for k in range(31):
        cst[:, C_CONVW + k * 8: C_CONVW + (k + 1) * 8] = _fm(cw[k])
    cst[:, C_CONVB:C_CONVB + 8] = _fm(inputs["conv_b"][0])
    cst[:, C_LNG:C_LNG + 8] = _fm(inputs["conv_ln_g"][0])
    cst[:, C_LNB:C_LNB + 8] = _fm(inputs["conv_ln_b"][0])
    sw = np.asarray(inputs["sc_conv_w"], np.float32)[0]
    for k in range(3):
        cst[:, C_SCW + k * 16: C_SCW + (k + 1) * 16] = _fm(sw[k])
    shared = {
        "ident": np.eye(128, dtype=np.float32),
        "pool_w": np.ascontiguousarray(np.asarray(inputs["pool_w"], np.float32)[0]),
        "ab_w_in": np.ascontiguousarray(np.asarray(inputs["ab_w_in"], np.float32)[0]),
        "ab_w_out": np.ascontiguousarray(np.asarray(inputs["ab_w_out"], np.float32)[0]),
        "sc_w_in": np.ascontiguousarray(np.asarray(inputs["sc_w_in"], np.float32)[0]),
        "sc_w_out": np.ascontiguousarray(np.asarray(inputs["sc_w_out"], np.float32)[0]),
    }
    for l in range(2):
        shared["ffn_w1_%d" % l] = np.ascontiguousarray(np.asarray(inputs["ffn_w1"], np.float32)[l])
        shared["ffn_w2_%d" % l] = np.ascontiguousarray(np.asarray(inputs["ffn_w2"], np.float32)[l])
    in_maps = []
    for core in range(NCORES):
        b, half = core // 2, core % 2
        xf = np.zeros((D, T), np.float32)
        if half == 0:
            xf[:, HALO:] = x[b, 0:TOWN, :].T
        else:
            xf[:, :] = x[b, TOWN - HALO:SEQ, :].T
        c = cst.copy()
        for g in range(4):
            w = 2 << g
            for i in range(16):
                c[:, C_INVFIX + g * 16 + i] = (1.0 / min(i + 1, w)) if half == 0 else (1.0 / w)
        c[:, C_MASK] = 0.0 if half == 0 else 1.0
        m = dict(shared)
        m["x_fm"] = xf
        m["cst"] = c
        in_maps.append(m)
    return in_maps


def assemble(results):
    out = np.zeros((4, SEQ, D), np.float32)
    for core in range(NCORES):
        b, half = core // 2, core % 2
        out[b, half * TOWN:(half + 1) * TOWN, :] = np.asarray(results[core]["out_fm"]).T
    return out


_NC_CACHE = []


def kernel(**inputs):
    if not _NC_CACHE:
        _NC_CACHE.append(build_program())
    nc = _NC_CACHE[0]
    in_maps = make_in_maps(inputs)
    res = run_bass_kernel_spmd(nc, in_maps, core_ids=list(range(NCORES)))
    return assemble(res.results)
```

```python
import bisect
from contextlib import ExitStack

import numpy as np
import concourse.bass as bass
import concourse.mybir as mybir
from concourse.bass_utils import run_bass_kernel_spmd

F32 = mybir.dt.float32
BF16 = mybir.dt.bfloat16
AF = mybir.ActivationFunctionType
ALU = mybir.AluOpType

D = 2048
KC = 16
SEQ = 2048
HALO = 32
TOWN = 1024
T = TOWN + HALO
EPS = 1e-6
NCORES = 8
CELL = 576
NCELL = 40
CW = 256
NSLOT = 4

PASSES = [
    (0, 544, [(0, 32, "H"), (32, 512, "M")]),
    (544, 512, [(0, 512, "M")]),
]

C_MIXPRE, C_MIXPOST, C_FFNPRE, C_FFNPOST = 0, 32, 64, 96
C_POOLSC = 128
C_CONVW = 136
C_CONVB = 384
C_LNG = 392
C_LNB = 400
C_SCW = 408
C_INVFIX = 456
C_MASK = 520
NCST = 528


class Op:
    __slots__ = ("gidx", "eng", "emit", "deps", "dsem", "flag", "tok", "incs", "waits", "pos")

    def __init__(self, gidx, eng, emit, deps, dsem, flag):
        self.gidx = gidx
        self.eng = eng
        self.emit = emit
        self.deps = deps
        self.dsem = dsem
        self.flag = flag
        self.tok = None
        self.incs = False
        self.waits = []
        self.pos = -1


class Builder:
    def __init__(self):
        self.ops = []
        self.state = {}
        self.acc_ctr = 0
        self.h_ctr = 0
        self.st_ctr = 0
        self.hs_ctr = 0
        self.slab_ctr = 0
        self.dma_counts = {}

    def add(self, eng, emit, reads=(), writes=(), dsem=None, flag=False, nowaw=False):
        deps = set()
        for k in reads:
            st = self.state.get(k)
            if st is not None and st[0] is not None:
                deps.add(st[0])
        for k in writes:
            st = self.state.get(k)
            if st is not None:
                if st[0] is not None and not (nowaw and self.ops[st[0]].eng == eng and self.ops[st[0]].dsem is None):
                    deps.add(st[0])
                deps.update(st[1].values())
                deps.update(st[2])
        g = len(self.ops)
        op = Op(g, eng, emit, deps, dsem, flag)
        self.ops.append(op)
        for k in reads:
            st = self.state.setdefault(k, [None, {}, []])
            if dsem is not None:
                st[2].append(g)
            else:
                st[1][eng] = g
        for k in writes:
            self.state[k] = [g, {}, []]
        return op

    def finalize(self, sems):
        ops = self.ops
        byeng = {}
        for op in ops:
            if op.dsem is None:
                lst = byeng.setdefault(op.eng, [])
                op.pos = len(lst)
                lst.append(op)
        pe_ops = byeng.get("pe", [])
        flagged = [op.pos for op in pe_ops if op.flag]
        for op in ops:
            if op.eng == "pe" and op.dsem is None:
                continue
            for d in op.deps:
                dop = ops[d]
                if dop.eng == "pe" and dop.dsem is None:
                    i = bisect.bisect_left(flagged, dop.pos)
                    if i < len(flagged) and pe_ops[flagged[i]].gidx < op.gidx:
                        continue
                    bisect.insort(flagged, dop.pos)
        flagged_set = set(flagged)
        for eng, lst in byeng.items():
            cnt = 0
            for op in lst:
                if eng != "pe" or op.pos in flagged_set:
                    cnt += 1
                    op.tok = (sems[eng], cnt)
                    op.incs = True
        dcount = {}
        for op in ops:
            if op.dsem is not None:
                dcount[op.dsem] = dcount.get(op.dsem, 0) + 1
                op.tok = (op.dsem, 16 * dcount[op.dsem])
                op.incs = True
        for op in ops:
            if op.dsem is not None and op.dsem[1] == "group":
                op.tok = (op.dsem, 16 * dcount[op.dsem])

        def token_of(dop, dependent_gidx):
            if dop.tok is not None:
                return dop.tok, dop
            i = bisect.bisect_left(flagged, dop.pos)
            f = pe_ops[flagged[i]]
            assert f.gidx < dependent_gidx
            return f.tok, f

        known = {}
        snaps = {}
        for op in ops:
            kn = known.setdefault(op.eng, {})
            for d in sorted(op.deps):
                dop = ops[d]
                if op.eng == "pe" and dop.eng == "pe" and dop.dsem is None and op.dsem is None:
                    continue
                tok, src = token_of(dop, op.gidx)
                sem, val = tok
                if kn.get(sem, 0) >= val:
                    continue
                op.waits.append((sem, val))
                kn[sem] = val
                sn = snaps.get(src.gidx)
                if sn:
                    for k, v in sn.items():
                        if kn.get(k, 0) < v:
                            kn[k] = v
            if op.incs:
                sn = dict(kn)
                if op.dsem is None:
                    sn[op.tok[0]] = max(sn.get(op.tok[0], 0), op.tok[1])
                else:
                    sn[op.tok[0]] = max(sn.get(op.tok[0], 0), op.tok[1])
                snaps[op.gidx] = sn
        return byeng


def build_program(debug=False, limit=None):
    nc = bass.Bass("TRN2", target_bir_lowering=False)
    B = Builder()
    dbg_d = nc.dram_tensor("dbg", [128, KC, 544], F32, kind="ExternalOutput").ap() if debug else None

    def din(name, shape):
        return nc.dram_tensor(name, list(shape), F32, kind="ExternalInput").ap()

    x_d = din("x_fm", (D, T))
    cst_d = din("cst", (128, NCST))
    poolw_d = din("pool_w", (4, 256, 256))
    ident_d = din("ident", (128, 128))
    w_abin = din("ab_w_in", (D, 3072))
    w_about = din("ab_w_out", (D, D))
    w_scin = din("sc_w_in", (D, 6144))
    w_scout = din("sc_w_out", (D, D))
    w_f1 = [din("ffn_w1_%d" % l, (D, 8192)) for l in range(2)]
    w_f2 = [din("ffn_w2_%d" % l, (8192, D)) for l in range(2)]
    out_d = nc.dram_tensor("out_fm", [D, TOWN], F32, kind="ExternalOutput").ap()

    es = ExitStack()
    with es:
        xs = es.enter_context(nc.sbuf_tensor("xs", [128, KC, T], F32))
        arena = es.enter_context(nc.sbuf_tensor("arena", [128, NCELL * CELL], F32))
        wbuf = es.enter_context(nc.sbuf_tensor("wbuf", [128, NSLOT, KC, CW], BF16))
        cst = es.enter_context(nc.sbuf_tensor("cst_sb", [128, NCST], F32))
        poolw = es.enter_context(nc.sbuf_tensor("poolw_sb", [128, 4, 2, 256], BF16))
        ones = es.enter_context(nc.sbuf_tensor("ones_sb", [128, 128], BF16))
        epsc = es.enter_context(nc.sbuf_tensor("eps_sb", [128, 1], F32))
        tail_glu = es.enter_context(nc.sbuf_tensor("tail_glu", [128, 8, 32], BF16))
        ident = es.enter_context(nc.sbuf_tensor("ident_sb", [128, 128], BF16))
        dg = es.enter_context(nc.sbuf_tensor("dg_sb", [128, 2, 16, 128], BF16))
        zeros = es.enter_context(nc.sbuf_tensor("zeros_sb", [128, 512], BF16))
        tail_u = es.enter_context(nc.sbuf_tensor("tail_u", [128, 8, 16], F32))
        tail_cu = es.enter_context(nc.sbuf_tensor("tail_cu", [128, 16, 2], F32))
        pacc = es.enter_context(nc.psum_tensor("pacc", [128, 3, 512], F32))
        ph = es.enter_context(nc.psum_tensor("ph", [128, 2, 512], F32))
        pst = es.enter_context(nc.psum_tensor("pst", [128, 1, 512], F32))
        pwarm = es.enter_context(nc.psum_tensor("pwarm", [128, 512], F32))
        phs = es.enter_context(nc.psum_tensor("phs", [128, 512], F32))

        sem_names = ["pe", "act", "dve", "pool", "sp"]
        sems = {}
        for n_ in sem_names:
            sems[n_] = es.enter_context(nc.semaphore("s_" + n_))
        dsem_handles = {}

        def dsem(name, mode="slot"):
            key = (name, mode)
            if key not in dsem_handles:
                dsem_handles[key] = es.enter_context(nc.semaphore("d_" + name))
            return key

        arena_bf = arena[:].bitcast(BF16)

        def cf(cell, a=0, b=CELL):
            return arena[:, cell * CELL + a: cell * CELL + b]

        def ch(hidx, a=0, b=CELL):
            return arena_bf[:, hidx * CELL + a: hidx * CELL + b]

        def kf(cell):
            return [("c", cell, 0), ("c", cell, 1)]

        def kh(hidx):
            return [("c", hidx // 2, hidx % 2)]

        HB0, Y0 = 0, 16
        MB0 = 16
        EX0 = 32

        def hb_ap(c, a, b):
            return ch(HB0 + c, a, b)

        def y_ap(c, a, b):
            return ch(Y0 + c, a, b)

        def mb_ap(c, a, b):
            return cf(MB0 + c, a, b)

        def cs(col):
            return cst[:, col:col + 1]

        def act(out, in_, func, reads, writes, bias=None, scale=None):
            kw = {}
            if bias is not None:
                kw["bias"] = bias
            if scale is not None:
                kw["scale"] = scale
            B.add("act", lambda e: e.activation(out=out, in_=in_, func=func, **kw), reads, writes)

        def tt(out, in0, in1, op, reads, writes, eng="dve"):
            B.add(eng, lambda e: e.tensor_tensor(out=out, in0=in0, in1=in1, op=op), reads, writes)

        def stt(out, in0, scalar, in1, op0, op1, reads, writes, eng="dve"):
            B.add(eng, lambda e: e.scalar_tensor_tensor(out=out, in0=in0, scalar=scalar, in1=in1,
                                                         op0=op0, op1=op1), reads, writes)

        def ts(out, in0, s1, s2, op0, op1, reads, writes, eng="dve"):
            if s2 is None:
                B.add(eng, lambda e: e.tensor_scalar(out=out, in0=in0, scalar1=s1, scalar2=None, op0=op0),
                      reads, writes)
            else:
                B.add(eng, lambda e: e.tensor_scalar(out=out, in0=in0, scalar1=s1, scalar2=s2,
                                                     op0=op0, op1=op1), reads, writes)

        def cp(out, in_, reads, writes, eng="dve"):
            B.add(eng, lambda e: e.tensor_copy(out=out, in_=in_), reads, writes)

        def mm(out, lhsT, rhs, start, stop, reads, writes, flag=False):
            B.add("pe", lambda e: e.matmul(out, lhsT, rhs, start=start, stop=stop), reads, writes, flag=flag)

        def warm(n):
            for _ in range(n):
                mm(pwarm[:, :], ones[:], zeros[:], True, True, ["ones", "zeros"], [])

        def dma(queue, out, in_, sem, reads, writes):
            B.add(queue, lambda e: e.dma_start(out=out, in_=in_), reads, writes, dsem=sem)

        B.add("dve", lambda e: e.memset(arena[:, 0:NCELL * CELL // 2], 0.0), [],
              [k for c in range(NCELL // 2) for k in kf(c)])
        B.add("dve", lambda e: e.memset(arena[:, NCELL * CELL // 2:], 0.0), [],
              [k for c in range(NCELL // 2, NCELL) for k in kf(c)])
        B.add("dve", lambda e: e.memset(ones[:], 1.0), [], ["ones"])
        B.add("dve", lambda e: e.memset(zeros[:], 0.0), [], ["zeros"])
        B.add("dve", lambda e: e.memset(epsc[:], EPS), [], ["cst"])
        B.add("dve", lambda e: e.memset(tail_glu[:], 0.0), [], [("tg", j) for j in range(8)])
        B.add("dve", lambda e: e.memset(tail_u[:], 0.0), [], [("tu", j) for j in range(8)])
        B.add("dve", lambda e: e.memset(tail_cu[:], 0.0), [], [("tc", j) for j in range(16)])
        sx = dsem("xin", "group")
        dma("sp", cst[:], cst_d, sx, [], ["cst"])
        xv = x_d.rearrange("(c p) t -> p c t", p=128)
        for q in range(4):
            dma("sp", xs[:, 4 * q:4 * q + 4, :], xv[:, 4 * q:4 * q + 4, :], sx, [],
                [("x", c, p) for c in range(4 * q, 4 * q + 4) for p in range(2)])
        dma("pool", poolw[:], poolw_d.rearrange("g (cc p) e -> p g cc e", p=128), dsem("pw", "group"), [], ["poolw"])
        dma("pool", ident[:], ident_d, dsem("pw", "group"), [], ["ident"])

        def tiles_for(l, p, full):
            if full or not (l == 1 and p == 0):
                return PASSES[p][2]
            return [t for t in PASSES[p][2] if t[2] == "M"]

        def trange(tiles):
            return min(t[0] for t in tiles), max(t[0] + t[1] for t in tiles)

        def alloc_acc(tiles):
            outs = []
            for (off, n, kind) in tiles:
                if kind == "M":
                    s = B.acc_ctr % 3
                    B.acc_ctr += 1
                    outs.append((pacc[:, s, 0:n], ("acc", s), off, n))
                else:
                    s = B.h_ctr % 2
                    B.h_ctr += 1
                    outs.append((ph[:, s, 0:n], ("ph", s), off, n))
            return outs

        def alloc_stat(tiles, alt=False):
            outs = []
            for (off, n, kind) in tiles:
                if kind == "M" and alt:
                    s = B.acc_ctr % 3
                    B.acc_ctr += 1
                    outs.append((pacc[:, s, 0:n], ("acc", s), off, n))
                elif kind == "M":
                    outs.append((pst[:, 0, 0:n], ("pst", 0), off, n))
                elif alt:
                    s = B.h_ctr % 2
                    B.h_ctr += 1
                    outs.append((ph[:, s, 0:n], ("ph", s), off, n))
                else:
                    outs.append((phs[:, 0:n], ("phs", 0), off, n))
            return outs

        def load_slab(W, r0, c0):
            slot = B.slab_ctr % NSLOT
            B.slab_ctr += 1
            src = W[r0:r0 + D, c0:c0 + CW].rearrange("(kc p) n -> p kc n", p=128)
            dma("pool", wbuf[:, slot, :, :], src, dsem("w%d" % slot), [], [("w", slot)])
            return slot

        def mm_stage(W, r0, cols, rhs_fn, rhs_keys, tiles, evac):
            deferred = []
            mi_g = 0
            for c0 in cols:
                slot = load_slab(W, r0, c0)
                for mi in range(CW // 128):
                    outs = alloc_acc(tiles(c0 + mi * 128) if callable(tiles) else tiles)
                    for k in range(KC):
                        for (oap, okey, off, n) in outs:
                            mm(oap, wbuf[:, slot, k, mi * 128:(mi + 1) * 128], rhs_fn(k, off, n),
                               k == 0, k == KC - 1, [("w", slot)] + rhs_keys(k), [okey], flag=(k == KC - 1))
                    nxt = evac(mi_g, c0 + mi * 128, outs) or []
                    for th in deferred:
                        th()
                    deferred = nxt
                    mi_g += 1
            for th in deferred:
                th()

        SQ = [2 * EX0, 2 * EX0 + 1, 2 * (EX0 + 1)]
        sq_ctr = [0]

        def next_sq():
            h = SQ[sq_ctr[0] % 3]
            sq_ctr[0] += 1
            return h

        def stat_accum(stat, src_h, first, last):
            for (oap, okey, off, n) in stat:
                mm(oap, ones[:], ch(src_h, off, off + n), first, last, ["ones"] + kh(src_h), [okey], flag=last)

        def rstd_from(stat, cell, scale, eps=EPS):
            for (oap, okey, off, n) in stat:
                act(cf(cell, off, off + n), oap, AF.Sqrt, [okey, "cst"], kf(cell), bias=epsc[:, 0:1], scale=scale)
            lo = min(t[2] for t in stat)
            hi = max(t[2] + t[3] for t in stat)
            B.add("dve", lambda e: e.reciprocal(out=cf(cell, lo, hi), in_=cf(cell, lo, hi)), kf(cell), kf(cell))

        def prenorm_stats(l, gbase, p, full=True, wpre=0, wmid=0, wpost=0):
            g0 = PASSES[p][0]
            tiles = tiles_for(l, p, full)
            lo, hi = trange(tiles)
            stat = alloc_stat(tiles)
            warm(wpre)
            for c in range(KC):
                h = next_sq()
                act(ch(h, lo, hi), xs[:, c, g0 + lo:g0 + hi], AF.Square, [("x", c, p)], kh(h))
                stat_accum(stat, h, c == 0, c == KC - 1)
                warm(wmid)
            warm(wpost)
            rstd_from(stat, EX0 + 3, 1.0 / D)

        def prenorm_hb(l, gbase, p, full=True):
            g0 = PASSES[p][0]
            lo, hi = trange(tiles_for(l, p, full))
            rc = EX0 + 3
            for c in range(KC):
                stt(hb_ap(c, lo, hi), xs[:, c, g0 + lo:g0 + hi], cs(gbase + l * 16 + c), cf(rc, lo, hi),
                    ALU.mult, ALU.mult, [("x", c, p), "cst"] + kf(rc), kh(HB0 + c))

        def prenorm_ffn(l, p):
            g0 = PASSES[p][0]
            tiles = tiles_for(l, p, False)
            lo, hi = trange(tiles)
            stat = alloc_stat(tiles)
            for c in range(KC):
                act(hb_ap(c, lo, hi), xs[:, c, g0 + lo:g0 + hi], AF.Identity, [("x", c, p), "cst"], kh(HB0 + c),
                    scale=cs(C_FFNPRE + l * 16 + c))
                h = next_sq()
                act(ch(h, lo, hi), xs[:, c, g0 + lo:g0 + hi], AF.Square, [("x", c, p)], kh(h))
                stat_accum(stat, h, c == 0, c == KC - 1)
            rstd_from(stat, EX0 + 3, 1.0 / D)

        def prenorm(l, gbase, p, full=True, wpre=0, wmid=0, wpost=0):
            prenorm_stats(l, gbase, p, full, wpre, wmid, wpost)
            prenorm_hb(l, gbase, p, full)

        def post_update(l, gbase, p, stat):
            g0 = PASSES[p][0]
            lo = min(t[2] for t in stat)
            hi = max(t[2] + t[3] for t in stat)
            rc = EX0 + 2
            rstd_from(stat, rc, 1.0 / D)
            for c in range(KC):
                tcell = EX0 + 4 + (c % 2)
                stt(cf(tcell, lo, hi), mb_ap(c, lo, hi), cs(gbase + l * 16 + c), cf(rc, lo, hi), ALU.mult, ALU.mult,
                    kf(MB0 + c) + ["cst"] + kf(rc), kf(tcell))
                tt(xs[:, c, g0 + lo:g0 + hi], cf(tcell, lo, hi), xs[:, c, g0 + lo:g0 + hi], ALU.add,
                   kf(tcell) + [("x", c, p)], [("x", c, p)])

        def hb_rhs(k, off, n):
            return hb_ap(k, off, off + n)

        def hb_keys(k):
            return kh(HB0 + k)

        def y_rhs(k, off, n):
            return y_ap(k, off, off + n)

        def y_keys(k):
            return kh(Y0 + k)

        def outproj(l, W, p):
            tiles = tiles_for(l, p, False)
            stat = alloc_stat(tiles)

            def evac(mi, c0, outs):
                h = next_sq()
                for (oap, okey, off, nn) in outs:
                    act(mb_ap(mi, off, off + nn), oap, AF.Copy, [okey], kf(MB0 + mi))
                    act(ch(h, off, off + nn), oap, AF.Square, [okey], kh(h))
                return [lambda: stat_accum(stat, h, mi == 0, mi == KC - 1)]

            mm_stage(W, 0, [b * CW for b in range(D // CW)], y_rhs, y_keys, tiles, evac)
            post_update(l, C_MIXPOST, p, stat)

        def ffn(l, p, nxt=None):
            tiles = tiles_for(l, p, False)
            lo, hi = trange(tiles)
            stat = alloc_stat(tiles)
            if nxt is not None:
                ng0 = PASSES[nxt[1]][0]
                ntiles = tiles_for(nxt[0], nxt[1], True)
                nlo, nhi = trange(ntiles)
            for q in range(4):
                if q == 2 and nxt is not None:
                    nstat = alloc_stat(ntiles)

                def evac1(mi, c0, outs, q=q):
                    rc_ = EX0 + 4 + (mi % 2)
                    for (oap, okey, off, nn) in outs:
                        act(cf(rc_, off, off + nn), oap, AF.Relu, [okey], kf(rc_))
                    tt(cf(rc_, lo, hi), cf(rc_, lo, hi), cf(EX0 + 3, lo, hi), ALU.mult, kf(rc_) + kf(EX0 + 3), kf(rc_))
                    tt(y_ap(mi, lo, hi), cf(rc_, lo, hi), cf(rc_, lo, hi), ALU.mult, kf(rc_), kh(Y0 + mi))
                    if q == 2 and nxt is not None:
                        h = next_sq()
                        act(ch(h, nlo, nhi), xs[:, mi, ng0 + nlo:ng0 + nhi], AF.Square, [("x", mi, nxt[1])], kh(h))
                        return [lambda: stat_accum(nstat, h, mi == 0, mi == KC - 1)]
                    return None

                mm_stage(w_f1[l], 0, [q * D + b * CW for b in range(D // CW)], hb_rhs, hb_keys, tiles, evac1)
                if q == 2 and nxt is not None:
                    rstd_from(nstat, EX0 + 7, 1.0 / D)

                def evac2(mi, c0, outs, q=q):
                    ths = None
                    for (oap, okey, off, nn) in outs:
                        if q == 0:
                            act(mb_ap(mi, off, off + nn), oap, AF.Copy, [okey], kf(MB0 + mi))
                        else:
                            tt(mb_ap(mi, off, off + nn), oap, mb_ap(mi, off, off + nn), ALU.add,
                               [okey] + kf(MB0 + mi), kf(MB0 + mi))
                    if q == 3:
                        h = next_sq()
                        act(ch(h, lo, hi), mb_ap(mi, lo, hi), AF.Square, kf(MB0 + mi), kh(h))
                        ths = [lambda: stat_accum(stat, h, mi == 0, mi == KC - 1)]
                        if nxt is not None:
                            stt(hb_ap(mi, nlo, nhi), xs[:, mi, ng0 + nlo:ng0 + nhi],
                                cs(C_MIXPRE + nxt[0] * 16 + mi), cf(EX0 + 7, nlo, nhi), ALU.mult, ALU.mult,
                                [("x", mi, nxt[1]), "cst"] + kf(EX0 + 7), kh(HB0 + mi))
                    return ths

                mm_stage(w_f2[l], q * D, [b * CW for b in range(D // CW)], y_rhs, y_keys, tiles, evac2)
            post_update(l, C_FFNPOST, p, stat)

        def mixer0(p):
            g0, n, tiles = PASSES[p]
            GLH = [2 * (MB0 + 12), 2 * (MB0 + 12) + 1]
            SG = [MB0 + 14, MB0 + 15]
            UA = EX0 + 6
            SA, SB = EX0 + 4, EX0 + 5

            def build_diag(j, hh):
                for sl in range(16):
                    k = hh * 16 + sl
                    if k > 30:
                        break
                    B.add("dve", (lambda e, sl=sl, k=k: e.tensor_scalar(
                        out=dg[:, hh, sl, :], in0=ident[:], scalar1=cs(C_CONVW + k * 8 + j), scalar2=None,
                        op0=ALU.mult)), ["ident", "cst"], [("dg", hh)], nowaw=True)

            def pooled_h(j):
                return 2 * (MB0 + 8) + j

            def evac(mi, c0, outs):
                ths = None
                if c0 >= 2048:
                    j = (c0 - 2048) // 128
                    sc = SG[j % 2]
                    for (oap, okey, off, nn) in outs:
                        act(cf(sc, off, off + nn), oap, AF.Sigmoid, [okey], kf(sc))
                elif c0 >= 1024:
                    j = (c0 - 1024) // 128
                    sc = SG[j % 2]
                    gh = GLH[j % 2]
                    if p == 1:
                        cp(ch(gh, 0, 32), tail_glu[:, j, :], [("tg", j)], kh(gh))
                    for (oap, okey, off, nn) in outs:
                        tt(ch(gh, 32 + off, 32 + off + nn), oap, cf(sc, off, off + nn), ALU.mult,
                           [okey] + kf(sc), kh(gh))
                    if p == 0:
                        cp(tail_glu[:, j, :], ch(gh, n, n + 32), kh(gh), [("tg", j)])
                    if j == 0:
                        build_diag(0, 0)
                        build_diag(0, 1)

                    def conv_mm(j=j, gh=gh):
                        outs2 = alloc_acc(tiles)
                        for k in range(31):
                            hh, sl = k // 16, k % 16
                            for (oap, okey, off, nn) in outs2:
                                mm(oap, dg[:, hh, sl, :], ch(gh, off + 2 + k, off + 2 + k + nn), k == 0, k == 30,
                                   [("dg", hh)] + kh(gh), [okey], flag=(k == 30 or k == 15))
                            if j < 7 and (k == 15 or k == 30):
                                build_diag(j + 1, hh)
                        for (oap, okey, off, nn) in outs2:
                            act(mb_ap(j, off, off + nn), oap, AF.Identity, [okey, "cst"], kf(MB0 + j),
                                bias=cs(C_CONVB + j))
                    ths = [conv_mm]
                else:
                    j = c0 // 128
                    g = j // 2
                    w = 2 << g
                    if p == 1:
                        cp(cf(UA, 0, 16), tail_u[:, j, :], [("tu", j)], kf(UA))
                    for (oap, okey, off, nn) in outs:
                        act(cf(UA, 16 + off, 16 + off + nn), oap, AF.Copy, [okey], kf(UA))
                    if p == 0:
                        cp(tail_u[:, j, :], cf(UA, n, n + 16), kf(UA), [("tu", j)])
                    src = UA
                    lo = 0
                    sh = 1
                    dsts = [SA, SB]
                    di = 0
                    while sh < w:
                        dst = dsts[di % 2]
                        di += 1
                        lo2 = lo + sh
                        tt(cf(dst, lo2, 16 + n), cf(src, lo2, 16 + n), cf(src, lo2 - sh, 16 + n - sh), ALU.add,
                           kf(src), kf(dst))
                        src = dst
                        lo = lo2
                        sh *= 2
                    hp = pooled_h(j)
                    stt(ch(hp, 0, n), cf(src, 16, 16 + n), 1.0 / w, cf(UA, 16, 16 + n), ALU.mult, ALU.subtract,
                        kf(src) + kf(UA), kh(hp))
                    if p == 0:
                        tmp = dsts[di % 2]
                        tt(cf(tmp, 0, 15), cf(src, 16 + 32, 16 + 47), cst[:, C_INVFIX + g * 16:C_INVFIX + g * 16 + 15],
                           ALU.mult, kf(src) + ["cst"], kf(tmp))
                        tt(ch(hp, 32, 47), cf(tmp, 0, 15), cf(UA, 16 + 32, 16 + 47), ALU.subtract,
                           kf(tmp) + kf(UA), kh(hp))
                    if j % 2 == 1:
                        def pool_mm(g=g):
                            for e in range(2):
                                outs2 = alloc_acc(tiles)
                                for cc in range(2):
                                    hpp = pooled_h(2 * g + cc)
                                    for (oap, okey, off, nn) in outs2:
                                        mm(oap, poolw[:, g, cc, e * 128:(e + 1) * 128], ch(hpp, off, off + nn),
                                           cc == 0, cc == 1, ["poolw"] + kh(hpp), [okey], flag=(cc == 1))
                                yi = 2 * g + e
                                for (oap, okey, off, nn) in outs2:
                                    act(y_ap(yi, off, off + nn), oap, AF.Identity, [okey, "cst"], kh(Y0 + yi),
                                        scale=cs(C_POOLSC + yi))
                        ths = [pool_mm]
                return ths

            cols = []
            for j2 in range(4):
                cols.append(2048 + j2 * CW)
                cols.append(1024 + j2 * CW)
            for j2 in range(4):
                cols.append(j2 * CW)
            mm_stage(w_abin, 0, cols, hb_rhs, hb_keys, tiles, evac)

            st_mean = alloc_stat(tiles)
            st_sq = alloc_stat(tiles, alt=True)
            for j in range(8):
                h1 = next_sq()
                act(ch(h1, 0, n), mb_ap(j, 0, n), AF.Copy, kf(MB0 + j), kh(h1))
                stat_accum(st_mean, h1, j == 0, j == 7)
                h2 = next_sq()
                act(ch(h2, 0, n), mb_ap(j, 0, n), AF.Square, kf(MB0 + j), kh(h2))
                stat_accum(st_sq, h2, j == 0, j == 7)
            MEAN, VAR, LR = EX0 + 2, EX0 + 3, EX0 + 7
            for (oap, okey, off, nn) in st_mean:
                act(cf(MEAN, off, off + nn), oap, AF.Copy, [okey], kf(MEAN), scale=1.0 / 1024)
            tt(cf(VAR, 0, n), cf(MEAN, 0, n), cf(MEAN, 0, n), ALU.mult, kf(MEAN), kf(VAR))
            for (oap, okey, off, nn) in st_sq:
                stt(cf(LR, off, off + nn), oap, 1.0 / 1024, cf(VAR, off, off + nn), ALU.mult, ALU.subtract,
                    [okey] + kf(VAR), kf(LR))
            act(cf(LR, 0, n), cf(LR, 0, n), AF.Sqrt, kf(LR) + ["cst"], kf(LR), bias=epsc[:, 0:1], scale=1.0)
            B.add("dve", lambda e: e.reciprocal(out=cf(LR, 0, n), in_=cf(LR, 0, n)), kf(LR), kf(LR))
            for j in range(8):
                tc_ = EX0 + 4 + (j % 2)
                tt(cf(tc_, 0, n), mb_ap(j, 0, n), cf(MEAN, 0, n), ALU.subtract, kf(MB0 + j) + kf(MEAN), kf(tc_))
                tt(cf(tc_, 0, n), cf(tc_, 0, n), cf(LR, 0, n), ALU.mult, kf(tc_) + kf(LR), kf(tc_))
                act(y_ap(8 + j, 0, n), cf(tc_, 0, n), AF.Silu, kf(tc_) + ["cst"], kh(Y0 + 8 + j),
                    bias=cs(C_LNB + j), scale=cs(C_LNG + j))

        def mixer1(p):
            g0, n, tiles = PASSES[p]
            CS = [MB0 + 0, MB0 + 1]
            CU = [MB0 + 2, MB0 + 3]
            CV = [MB0 + 4, MB0 + 5]

            def evac(mi, c0, outs):
                if c0 >= 4096:
                    j = (c0 - 4096) // 128
                    cc, cu, cv = CS[j % 2], CU[j % 2], CV[j % 2]
                    if p == 1:
                        cp(cf(cu, 0, 2), tail_cu[:, j, :], [("tc", j)], kf(cu))
                    for (oap, okey, off, nn) in outs:
                        tt(cf(cu, 2 + off, 2 + off + nn), oap, cf(cc, off, off + nn), ALU.mult,
                           [okey] + kf(cc), kf(cu))
                    if p == 0:
                        cp(tail_cu[:, j, :], cf(cu, n, n + 2), kf(cu), [("tc", j)])
                    ts(cf(cv, 0, n), cf(cu, 0, n), cs(C_SCW + j), None, ALU.mult, None, kf(cu) + ["cst"], kf(cv))
                    for k in range(1, 3):
                        stt(cf(cv, 0, n), cf(cu, k, k + n), cs(C_SCW + k * 16 + j), cf(cv, 0, n),
                            ALU.mult, ALU.add, kf(cu) + kf(cv) + ["cst"], kf(cv))
                elif c0 >= 2048:
                    j = (c0 - 2048) // 128
                    cc = CS[j % 2]
                    for (oap, okey, off, nn) in outs:
                        act(cf(cc, off, off + nn), oap, AF.Copy, [okey], kf(cc))
                else:
                    j = c0 // 128
                    cv = CV[j % 2]
                    for (oap, okey, off, nn) in outs:
                        tt(y_ap(j, off, off + nn), oap, cf(cv, off, off + nn), ALU.mult, [okey] + kf(cv),
                           kh(Y0 + j))
                return None

            cols = []
            for j2 in range(8):
                cols.append(2048 + j2 * CW)
                cols.append(4096 + j2 * CW)
                cols.append(j2 * CW)
            mtiles = [t for t in tiles if t[2] == "M"]
            mm_stage(w_scin, 0, cols, hb_rhs, hb_keys, (lambda c0: mtiles if c0 < 2048 else tiles), evac)

        so = dsem("out", "group")
        ov = out_d.rearrange("(c p) t -> p c t", p=128)
        def reached(l, p, st):
            return limit is not None and (l, p, st) > tuple(limit)

        for l in range(2):
            for p in range(2):
                g0, n, tiles = PASSES[p]
                if l == 0 and p == 0:
                    prenorm(l, C_MIXPRE, p)
                if not reached(l, p, 1):
                    if l == 0:
                        mixer0(p)
                    else:
                        mixer1(p)
                    if debug and p == 0 and l == 0:
                        for c in range(KC):
                            dma("pool", dbg_d[:, c, :], y_ap(c, 0, 544), dsem("dbg", "group"), kh(Y0 + c), [("dbg", c)])
                if not reached(l, p, 2):
                    outproj(l, w_about if l == 0 else w_scout, p)
                if not reached(l, p, 3):
                    prenorm_ffn(l, p)
                if not reached(l, p, 4):
                    nxt = (l, 1) if p == 0 else ((l + 1, 0) if l == 0 else None)
                    ffn(l, p, nxt)
                if l == 0 and p == 0:
                    ts(xs[:, :, 0:HALO], xs[:, :, 0:HALO], cs(C_MASK), None, ALU.mult, None,
                       [("x", c, 0) for c in range(KC)] + ["cst"], [("x", c, 0) for c in range(KC)])
                if l == 1:
                    lo = HALO if p == 0 else g0
                    for q in range(4):
                        dma("sp", ov[:, 4 * q:4 * q + 4, lo - HALO:lo - HALO + 512], xs[:, 4 * q:4 * q + 4, lo:lo + 512],
                            so, [("x", c, p) for c in range(4 * q, 4 * q + 4)], [("out", p, q)])
        B.add("sp", None, [("out", p, q) for p in range(2) for q in range(4)]
              + ([("dbg", c) for c in range(KC)] if debug else []), ["done"])

        byeng = B.finalize({k: k for k in sem_names})

        def sem_of(s):
            if isinstance(s, tuple):
                return dsem_handles[s]
            return sems[s]

        def emit_engine(name, e):
            lst = [op for op in B.ops if op.eng == name]
            for op in lst:
                for (s, v) in op.waits:
                    e.wait_ge(sem_of(s), v)
                if op.emit is None:
                    continue
                ins = op.emit(e)
                if op.incs:
                    if op.dsem is not None:
                        ins.then_inc(sem_of(op.dsem), 16)
                    else:
                        ins.then_inc(sems[name], 1)

        with nc.Block() as block:
            @block.tensor
            def _(e):
                emit_engine("pe", e)

            @block.scalar
            def _(e):
                emit_engine("act", e)

            @block.vector
            def _(e):
                emit_engine("dve", e)

            @block.gpsimd
            def _(e):
                emit_engine("pool", e)

            @block.sync
            def _(e):
                emit_engine("sp", e)
    return nc


def _fm(v):
    v = np.asarray(v, dtype=np.float32)
    return np.ascontiguousarray(v.reshape(-1, 128).T)


def make_in_maps(inputs):
    x = np.asarray(inputs["x"], dtype=np.float32)
    cst = np.zeros((128, NCST), np.float32)
    for base, name in ((C_MIXPRE, "mix_pre_g"), (C_MIXPOST, "mix_post_g"), (C_FFNPRE, "ffn_pre_g"),
                       (C_FFNPOST, "ffn_post_g")):
        g = np.asarray(inputs[name], np.float32)
        for l in range(2):
            cst[:, base + l * 16: base + (l + 1) * 16] = _fm(g[l])
    cst[:, C_POOLSC:C_POOLSC + 8] = _fm(inputs["pool_scale"][0])
    cw = np.asarray(inputs["conv_w"], np.float32)[0]
    for k in range(31):
        cst[:, C_CONVW + k * 8: C_CONVW + (k + 1) * 8] = _fm(cw[k])
    cst[:, C_CONVB:C_CONVB + 8] = _fm(inputs["conv_b"][0])
    cst[:, C_LNG:C_LNG + 8] = _fm(inputs["conv_ln_g"][0])
    cst[:, C_LNB:C_LNB + 8] = _fm(inputs["conv_ln_b"][0])
    sw = np.asarray(inputs["sc_conv_w"], np.float32)[0]
    for k in range(3):
        cst[:, C_SCW + k * 16: C_SCW + (k + 1) * 16] = _fm(sw[k])
    shared = {
        "ident": np.eye(128, dtype=np.float32),
        "pool_w": np.ascontiguousarray(np.asarray(inputs["pool_w"], np.float32)[0]),
        "ab_w_in": np.ascontiguousarray(np.asarray(inputs["ab_w_in"], np.float32)[0]),
        "ab_w_out": np.ascontiguousarray(np.asarray(inputs["ab_w_out"], np.float32)[0]),
        "sc_w_in": np.ascontiguousarray(np.asarray(inputs["sc_w_in"], np.float32)[0]),
        "sc_w_out": np.ascontiguousarray(np.asarray(inputs["sc_w_out"], np.float32)[0]),
    }
    for l in range(2):
        shared["ffn_w1_%d" % l] = np.ascontiguousarray(np.asarray(inputs["ffn_w1"], np.float32)[l])
        shared["ffn_w2_%d" % l] = np.ascontiguousarray(np.asarray(inputs["ffn_w2"], np.float32)[l])
    in_maps = []
    for core in range(NCORES):
        b, half = core // 2, core % 2
        xf = np.zeros((D, T), np.float32)
        if half == 0:
            xf[:, HALO:] = x[b, 0:TOWN, :].T
        else:
            xf[:, :] = x[b, TOWN - HALO:SEQ, :].T
        c = cst.copy()
        for g in range(4):
            w = 2 << g
            for i in range(16):
                c[:, C_INVFIX + g * 16 + i] = (1.0 / min(i + 1, w)) if half == 0 else (1.0 / w)
        c[:, C_MASK] = 0.0 if half == 0 else 1.0
        m = dict(shared)
        m["x_fm"] = xf
        m["cst"] = c
        in_maps.append(m)
    return in_maps


def assemble(results):
    out = np.zeros((4, SEQ, D), np.float32)
    for core in range(NCORES):
        b, half = core // 2, core % 2
        out[b, half * TOWN:(half + 1) * TOWN, :] = np.asarray(results[core]["out_fm"]).T
    return out


_NC_CACHE = []


def kernel(**inputs):
    if not _NC_CACHE:
        _NC_CACHE.append(build_program())
    nc = _NC_CACHE[0]
    in_maps = make_in_maps(inputs)
    res = run_bass_kernel_spmd(nc, in_maps, core_ids=list(range(NCORES)))
    return assemble(res.results)
```
